# Optimizing a Trainium2 kernel written in Bass

```python
import math
import jax, jax.numpy as jnp
from jax import lax
import numpy as np

D_MODEL = 1024
BATCH = 2
SEQ = 8192
DEPTH = 4

CTX_LEN = 256
GRID_W = 64
EPS = 1e-6
ROPE_BASE = 10000.0
ROPE_DIM = 32
Q_BLOCK = 128
CHUNK = 64

MLA_HEADS = 4
MLA_Q_RANK = 256
MLA_KV_RANK = 128
MLA_NOPE = 64
MLA_ROPE = ROPE_DIM
MLA_V = 64
MLA_SCALE = (MLA_NOPE + MLA_ROPE) ** -0.5

DIFF_HEADS = 4
DIFF_DK = ROPE_DIM
DIFF_DV = 2 * DIFF_DK
DIFF_SCALE = DIFF_DK ** -0.5

HGRN_HEADS = 8
HGRN_DK = 64
HGRN_DV = 64

MLA_WIDTH = MLA_HEADS * MLA_V
DIFF_WIDTH = DIFF_HEADS * DIFF_DV
HGRN_WIDTH = HGRN_HEADS * HGRN_DV
MIX_WIDTH = MLA_WIDTH + DIFF_WIDTH + HGRN_WIDTH

D_FF = (((8 * D_MODEL + 2) // 3 + 255) // 256) * 256

IN_SIZES = (
    MLA_Q_RANK, MLA_KV_RANK, MLA_ROPE,
    DIFF_HEADS * 2 * DIFF_DK, DIFF_HEADS * 2 * DIFF_DK, DIFF_WIDTH,
    HGRN_HEADS * HGRN_DK, HGRN_HEADS * HGRN_DK, HGRN_HEADS * HGRN_DK,
    HGRN_WIDTH, HGRN_WIDTH,
)
IN_WIDTH = sum(IN_SIZES)
IN_OFFSETS = tuple(int(o) for o in np.cumsum(IN_SIZES)[:-1])

kernel_name = 'hybrid_mla_diffattn_hgrn2_flow_block'


def rms_norm(x, g):
    xf = x.astype(jnp.float32)
    y = xf * lax.rsqrt(jnp.mean(xf * xf, axis=-1, keepdims=True) + EPS)
    return (y * g.astype(jnp.float32)).astype(x.dtype)


def modulate(x, shift, scale):
    return x * (1 + scale) + shift


def heads(a, n_heads):
    b, t, _ = a.shape
    return a.reshape(b, t, n_heads, -1).transpose(0, 2, 1, 3)


def merge_heads(a):
    b, h, t, d = a.shape
    return a.transpose(0, 2, 1, 3).reshape(b, t, h * d)


def axial_rope_tables(rows):
    t = jnp.arange(rows * GRID_W)
    row = (t // GRID_W).astype(jnp.float32)
    col = (t % GRID_W).astype(jnp.float32)
    n_freq = ROPE_DIM // 4
    freqs = ROPE_BASE ** (-jnp.arange(n_freq, dtype=jnp.float32) / n_freq)
    ang_r = row[:, None] * freqs
    ang_c = col[:, None] * freqs
    return (jnp.cos(ang_r), jnp.sin(ang_r), jnp.cos(ang_c), jnp.sin(ang_c))


def rope_1d(x, cos, sin):
    x1, x2 = jnp.split(x, 2, axis=-1)
    cos = cos.astype(x.dtype)
    sin = sin.astype(x.dtype)
    return jnp.concatenate([x1 * cos - x2 * sin, x2 * cos + x1 * sin], axis=-1)


def rope_2d(x, rope):
    cos_r, sin_r, cos_c, sin_c = rope
    x_row, x_col = jnp.split(x, 2, axis=-1)
    return jnp.concatenate([rope_1d(x_row, cos_r, sin_r), rope_1d(x_col, cos_c, sin_c)], axis=-1)


def layer_lower_bounds(raw):
    p = jax.nn.softmax(raw.astype(jnp.float32), axis=0)
    cum = jnp.cumsum(p, axis=0)
    return cum - cum[0:1]


def log_forget(z, lb):
    z = z.astype(jnp.float32)
    return jnp.logaddexp(jnp.log(lb), jnp.log1p(-lb) + jax.nn.log_sigmoid(z))


def sweep_query_blocks(fn, *qs):
    b, h, t = qs[0].shape[:3]
    nb = t // Q_BLOCK
    blocks = tuple(jnp.moveaxis(a.reshape(b, h, nb, Q_BLOCK, a.shape[-1]), 2, 0) for a in qs)
    o = lax.map(lambda xs: fn(*xs), blocks)
    return jnp.moveaxis(o, 0, 2).reshape(b, h, t, o.shape[-1])


def softmax_attention(q, k, v, scale):
    def one(qb):
        s = jnp.einsum('bhqd,bhkd->bhqk', qb, k).astype(jnp.float32) * scale
        p = jax.nn.softmax(s, axis=-1).astype(v.dtype)
        return jnp.einsum('bhqk,bhkd->bhqd', p, v)
    return sweep_query_blocks(one, q)


def differential_attention(q1, q2, k1, k2, v, lam):
    def one(qb1, qb2):
        s1 = jnp.einsum('bhqd,bhkd->bhqk', qb1, k1).astype(jnp.float32) * DIFF_SCALE
        s2 = jnp.einsum('bhqd,bhkd->bhqk', qb2, k2).astype(jnp.float32) * DIFF_SCALE
        p = jax.nn.softmax(s1, axis=-1) - lam * jax.nn.softmax(s2, axis=-1)
        return jnp.einsum('bhqk,bhkd->bhqd', p.astype(v.dtype), v)
    return sweep_query_blocks(one, q1, q2)


def diff_head_out(o, g, lam_init):
    return merge_heads(rms_norm(o, g) * (1 - lam_init))


def gla_chunk_scan(q, k, v, log_f, s0):
    b, h, t, dk = q.shape
    n = t // CHUNK

    def chunks(a):
        return jnp.moveaxis(a.reshape(b, h, n, CHUNK, a.shape[-1]), 2, 0)

    incl = jnp.tril(jnp.ones((CHUNK, CHUNK), dtype=bool))[:, :, None]

    def step(state, xs):
        qc, kc, vc, gc = xs
        cum = jnp.cumsum(gc, axis=2)
        o_inter = jnp.einsum('bhtk,bhkv->bhtv', qc * jnp.exp(cum), state)
        rel = cum[:, :, :, None, :] - cum[:, :, None, :, :]
        decay = jnp.exp(jnp.where(incl, rel, -jnp.inf))
        scores = jnp.einsum('bhtk,bhsk,bhtsk->bhts', qc, kc, decay)
        o_intra = jnp.einsum('bhts,bhsv->bhtv', scores, vc)
        last = cum[:, :, -1, :]
        k_to_end = kc * jnp.exp(last[:, :, None, :] - cum)
        new_state = jnp.exp(last)[..., None] * state + jnp.einsum('bhsk,bhsv->bhkv', k_to_end, vc)
        return new_state, o_inter + o_intra

    final, o = lax.scan(step, s0, (chunks(q), chunks(k), chunks(v), chunks(log_f)))
    o = jnp.moveaxis(o, 0, 2).reshape(b, h, t, v.shape[-1])
    return o, final


def hgrn_bidirectional(f, s0_fwd, s0_bwd):
    q, v = f['hgrn_q'], f['hgrn_v']
    lf_f, lf_b = f['hgrn_lf_fwd'], f['hgrn_lf_bwd']
    o_f, s_f = gla_chunk_scan(q, -jnp.expm1(lf_f), v, lf_f, s0_fwd)
    flip = lambda a: jnp.flip(a, axis=2)
    o_b, s_b = gla_chunk_scan(flip(q), flip(-jnp.expm1(lf_b)), flip(v), flip(lf_b), s0_bwd)
    return o_f + flip(o_b), s_f, s_b


def hgrn_head_out(o, g, gain):
    return merge_heads(rms_norm(o, gain)).astype(g.dtype) * jax.nn.silu(g)


def stream_features(h, lw, rope):
    proj = h @ lw['w_in']
    cq, ckv, kr, dq, dk, dv, hq, hf_fwd, hf_bwd, hi, hg = jnp.split(proj, IN_OFFSETS, axis=-1)
    b, t, _ = h.shape
    q = heads(rms_norm(cq, lw['g_q_norm']) @ lw['w_uq'], MLA_HEADS)
    kv = heads(rms_norm(ckv, lw['g_kv_norm']) @ lw['w_ukv'], MLA_HEADS)
    q_nope, q_rope = jnp.split(q, [MLA_NOPE], axis=-1)
    k_nope, mla_v = jnp.split(kv, [MLA_NOPE], axis=-1)
    k_rope = kr[:, None]
    dq = dq.reshape(b, t, DIFF_HEADS, 2, DIFF_DK).transpose(0, 2, 3, 1, 4)
    dk = dk.reshape(b, t, DIFF_HEADS, 2, DIFF_DK).transpose(0, 2, 3, 1, 4)
    if rope is not None:
        q_rope = rope_2d(q_rope, rope)
        k_rope = rope_2d(k_rope, rope)
        dq = rope_2d(dq, rope)
        dk = rope_2d(dk, rope)
    mla_q = jnp.concatenate([q_nope, q_rope], axis=-1)
    mla_k = jnp.concatenate([k_nope, jnp.broadcast_to(k_rope, (b, MLA_HEADS, t, MLA_ROPE))], axis=-1)
    lf_fwd = heads(log_forget(hf_fwd, lw['lb'][0]), HGRN_HEADS)
    lf_bwd = heads(log_forget(hf_bwd, lw['lb'][1]), HGRN_HEADS)
    return {
        'mla_q': mla_q, 'mla_k': mla_k, 'mla_v': mla_v,
        'diff_q1': dq[:, :, 0], 'diff_q2': dq[:, :, 1],
        'diff_k1': dk[:, :, 0], 'diff_k2': dk[:, :, 1],
        'diff_v': heads(dv, DIFF_HEADS),
        'hgrn_q': heads(jax.nn.silu(hq), HGRN_HEADS).astype(jnp.float32),
        'hgrn_v': heads(hi, HGRN_HEADS).astype(jnp.float32),
        'hgrn_lf_fwd': lf_fwd, 'hgrn_lf_bwd': lf_bwd,
        'hgrn_g': hg,
    }


def token_mixers(fl, fc, lw, lam, lam_init, with_ctx_out):
    cat = lambda a, b_: jnp.concatenate([a, b_], axis=2)
    mla_l = merge_heads(softmax_attention(fl['mla_q'], cat(fc['mla_k'], fl['mla_k']),
                                          cat(fc['mla_v'], fl['mla_v']), MLA_SCALE))
    diff_l = diff_head_out(differential_attention(
        fl['diff_q1'], fl['diff_q2'], cat(fc['diff_k1'], fl['diff_k1']), cat(fc['diff_k2'], fl['diff_k2']),
        cat(fc['diff_v'], fl['diff_v']), lam), lw['g_diff_norm'], lam_init)
    b = fc['hgrn_q'].shape[0]
    zeros = jnp.zeros((b, HGRN_HEADS, HGRN_DK, HGRN_DV), jnp.float32)
    o_c, s_cf, s_cb = hgrn_bidirectional(fc, zeros, zeros)
    o_l, _, _ = hgrn_bidirectional(fl, s_cf, s_cb)
    hgrn_l = hgrn_head_out(o_l, fl['hgrn_g'], lw['g_hgrn_norm'])
    mix_l = jnp.concatenate([mla_l, diff_l, hgrn_l], axis=-1)
    if not with_ctx_out:
        return mix_l, None
    mla_c = merge_heads(softmax_attention(fc['mla_q'], fc['mla_k'], fc['mla_v'], MLA_SCALE))
    diff_c = diff_head_out(differential_attention(
        fc['diff_q1'], fc['diff_q2'], fc['diff_k1'], fc['diff_k2'], fc['diff_v'], lam),
        lw['g_diff_norm'], lam_init)
    hgrn_c = hgrn_head_out(o_c, fc['hgrn_g'], lw['g_hgrn_norm'])
    mix_c = jnp.concatenate([mla_c, diff_c, hgrn_c], axis=-1)
    return mix_l, mix_c


def swiglu(h, w_gate, w_up, w_down):
    return (jax.nn.silu(h @ w_gate) * (h @ w_up)) @ w_down


def setup_inputs(seed: int = 0) -> dict:
    key = jax.random.key(seed)
    ks = jax.random.split(key, 24)
    f32 = jnp.float32

    def nrm(k, shape, scale):
        return jax.random.normal(k, shape, f32) * scale

    def gain(k, shape):
        return 1.0 + 0.05 * jax.random.normal(k, shape, f32)

    return {
        'x': nrm(ks[0], (BATCH, SEQ, D_MODEL), 1.0),
        'c': nrm(ks[1], (BATCH, D_MODEL), 1.0),
        'ctx': nrm(ks[2], (BATCH, CTX_LEN, D_MODEL), 1.0),
        'c_ctx': nrm(ks[3], (D_MODEL,), 1.0),
        'w_ada': nrm(ks[4], (DEPTH, D_MODEL, 6 * D_MODEL), 0.5 * D_MODEL ** -0.5),
        'b_ada': nrm(ks[5], (DEPTH, 6 * D_MODEL), 0.02),
        'g_norm1': gain(ks[6], (DEPTH, D_MODEL)),
        'g_norm2': gain(ks[7], (DEPTH, D_MODEL)),
        'w_in': nrm(ks[8], (DEPTH, D_MODEL, IN_WIDTH), D_MODEL ** -0.5),
        'g_q_norm': gain(ks[9], (DEPTH, MLA_Q_RANK)),
        'w_uq': nrm(ks[10], (DEPTH, MLA_Q_RANK, MLA_HEADS * (MLA_NOPE + MLA_ROPE)), MLA_Q_RANK ** -0.5),
        'g_kv_norm': gain(ks[11], (DEPTH, MLA_KV_RANK)),
        'w_ukv': nrm(ks[12], (DEPTH, MLA_KV_RANK, MLA_HEADS * (MLA_NOPE + MLA_V)), MLA_KV_RANK ** -0.5),
        'diff_lambda': nrm(ks[13], (DEPTH, 4, DIFF_DK), 0.1),
        'g_diff_norm': gain(ks[14], (DEPTH, DIFF_DV)),
        'hgrn_lower_bounds': nrm(ks[15], (DEPTH, 2, HGRN_HEADS * HGRN_DK), 0.1),
        'g_hgrn_norm': gain(ks[16], (DEPTH, HGRN_DV)),
        'w_out': nrm(ks[17], (DEPTH, MIX_WIDTH, D_MODEL), MIX_WIDTH ** -0.5),
        'w_ffn_gate': nrm(ks[18], (DEPTH, D_MODEL, D_FF), D_MODEL ** -0.5),
        'w_ffn_up': nrm(ks[19], (DEPTH, D_MODEL, D_FF), D_MODEL ** -0.5),
        'w_ffn_down': nrm(ks[20], (DEPTH, D_FF, D_MODEL), D_FF ** -0.5),
        'g_final': gain(ks[21], (D_MODEL,)),
    }


def reference(x, c, ctx, c_ctx, w_ada, b_ada, g_norm1, g_norm2, w_in, g_q_norm, w_uq, g_kv_norm, w_ukv,
              diff_lambda, g_diff_norm, hgrn_lower_bounds, g_hgrn_norm, w_out, w_ffn_gate, w_ffn_up,
              w_ffn_down, g_final):
    ROWS = x.shape[1] // GRID_W
    rope = axial_rope_tables(ROWS)
    lb_all = layer_lower_bounds(hgrn_lower_bounds)
    silu_c = jax.nn.silu(c)
    silu_cc = jax.nn.silu(c_ctx)
    for l in range(DEPTH):
        with_ctx_out = l < DEPTH - 1
        mod_l = silu_c @ w_ada[l] + b_ada[l]
        mod_c = silu_cc @ w_ada[l] + b_ada[l]
        sh1, sc1, gt1, sh2, sc2, gt2 = jnp.split(mod_l[:, None, :], 6, axis=-1)
        csh1, csc1, cgt1, csh2, csc2, cgt2 = jnp.split(mod_c, 6, axis=-1)
        lw = {
            'w_in': w_in[l], 'g_q_norm': g_q_norm[l], 'w_uq': w_uq[l], 'g_kv_norm': g_kv_norm[l],
            'w_ukv': w_ukv[l], 'lb': lb_all[l], 'g_diff_norm': g_diff_norm[l], 'g_hgrn_norm': g_hgrn_norm[l],
        }
        lq1, lk1, lq2, lk2 = diff_lambda[l].astype(jnp.float32)
        lam_init = 0.8 - 0.6 * math.exp(-0.3 * l)
        lam = jnp.exp(jnp.sum(lq1 * lk1)) - jnp.exp(jnp.sum(lq2 * lk2)) + lam_init

        h = modulate(rms_norm(x, g_norm1[l]), sh1, sc1)
        hc = modulate(rms_norm(ctx, g_norm1[l]), csh1, csc1)
        fl = stream_features(h, lw, rope)
        fc = stream_features(hc, lw, None)
        mix_l, mix_c = token_mixers(fl, fc, lw, lam, lam_init, with_ctx_out)
        x = x + gt1 * (mix_l @ w_out[l])
        h2 = modulate(rms_norm(x, g_norm2[l]), sh2, sc2)
        x = x + gt2 * swiglu(h2, w_ffn_gate[l], w_ffn_up[l], w_ffn_down[l])
        if with_ctx_out:
            ctx = ctx + cgt1 * (mix_c @ w_out[l])
            h2c = modulate(rms_norm(ctx, g_norm2[l]), csh2, csc2)
            ctx = ctx + cgt2 * swiglu(h2c, w_ffn_gate[l], w_ffn_up[l], w_ffn_down[l])
    return rms_norm(x, g_final)
```

```python
import math
from contextlib import ExitStack

import numpy as np
import ml_dtypes

import concourse.bass as bass
import concourse.mybir as mybir
from concourse.bass_utils import run_bass_kernel_spmd

F32 = mybir.dt.float32
BF16 = mybir.dt.bfloat16
AF = mybir.ActivationFunctionType
ALU = mybir.AluOpType
AX = mybir.AxisListType

NCORES = 8
D = 1024
DEPTH = 4
SEQ = 8192
CTX = 256
LAT = 1024
CT = 32
NT = 2 * LAT + 2 * CT
HT = LAT + CT
RW = LAT + 64
DFF = 2816
NFF = DFF // 128
EPS = 1e-6
VW = 66
INW = 3744
O_CQ, O_CKV, O_KR, O_DQ, O_DK, O_DV, O_HQ, O_HFF, O_HFB, O_HI, O_HG = 0, 256, 384, 416, 672, 928, 1184, 1696, 2208, 2720, 3232
MLA_SCALE = 96 ** -0.5
DIFF_SCALE = 32 ** -0.5

KTM_SZ = 4 * 96 * NT
VM_SZ = 4 * 128 * 17 * VW
KTD_SZ = 2 * 128 * NT
VD_SZ = VM_SZ
OFF_KTM, OFF_VM, OFF_KTD, OFF_VD = 0, KTM_SZ, KTM_SZ + VM_SZ, KTM_SZ + VM_SZ + KTD_SZ
EXK_SZ = OFF_VD + VD_SZ
EXS_W = 8 * 4 * 65

ENGS = ("sync", "scalar", "vector", "gpsimd", "tensor")


class Chan:
    __slots__ = ("name", "sem", "count")

    def __init__(self, name, sem):
        self.name, self.sem, self.count = name, sem, 0


class Prog:
    def __init__(self, nc):
        self.nc = nc
        self.ops = {e: [] for e in ENGS}
        self.waited = {e: {} for e in ENGS}
        self.chans = {}
        self.lastw = {}
        self.readers = {}
        self._sems = []
        self.dma_rr = {"sync": 0, "gpsimd": 0}
        self.nops = 0

    def chan(self, name):
        if name not in self.chans:
            cm = self.nc.semaphore("s_" + name)
            sem = cm.__enter__()
            self._sems.append(cm)
            self.chans[name] = Chan(name, sem)
        return self.chans[name]

    def op(self, eng, fn, reads=(), writes=(), dma=False):
        if dma:
            i = self.dma_rr[eng]
            self.dma_rr[eng] = (i + 1) % 8
            chan = "d_%s%d" % (eng, i)
        else:
            chan = "e_" + eng
        ch = self.chan(chan)
        deps = {}

        def add(d):
            if d is not None and deps.get(d[0], 0) < d[1]:
                deps[d[0]] = d[1]

        for k in reads:
            add(self.lastw.get(k))
        for k in writes:
            add(self.lastw.get(k))
            for r in self.readers.get(k, ()):
                add(r)
        if dma and ch.count > 0:
            add((chan, ch.count))
        waits = []
        own = "e_" + eng
        for c, v in deps.items():
            if c == own and eng == "tensor":
                continue
            if self.waited[eng].get(c, 0) < v:
                waits.append((self.chans[c].sem, v))
                self.waited[eng][c] = v
        inc = 16 if dma else 1
        ch.count += inc
        me = (chan, ch.count)
        self.ops[eng].append((waits, fn, ch.sem, inc))
        for k in writes:
            self.lastw[k] = me
            self.readers[k] = []
        for k in reads:
            self.readers.setdefault(k, []).append(me)
        self.nops += 1
        return me

    def barrier(self):
        for e in ENGS:
            waits = []
            for c, ch in self.chans.items():
                if ch.count > 0 and self.waited[e].get(c, 0) < ch.count and not (c == "e_tensor" and e == "tensor"):
                    waits.append((ch.sem, ch.count))
                    self.waited[e][c] = ch.count
            if waits:
                self.ops[e].append((waits, None, None, 0))
        self.lastw = {}
        self.readers = {}

    def finish(self, eng="sync"):
        waits = []
        for c, ch in self.chans.items():
            if ch.count > 0 and self.waited[eng].get(c, 0) < ch.count:
                waits.append((ch.sem, ch.count))
                self.waited[eng][c] = ch.count
        if waits:
            self.ops[eng].append((waits, None, None, 0))

    def emit(self):
        nc = self.nc
        with nc.Block() as block:
            for e in ENGS:
                lst = self.ops[e]

                def body(engine, lst=lst):
                    for waits, fn, sem, inc in lst:
                        for s, v in waits:
                            engine.wait_ge(s, v)
                        if fn is not None:
                            fn(engine).then_inc(sem, inc)

                getattr(block, e)(body)
        for cm in reversed(self._sems):
            cm.__exit__(None, None, None)


class StopBuild(Exception):
    pass


class K:
    pass


def sbt(k, name, shape, dt):
    return k.es.enter_context(k.nc.sbuf_tensor("sb_" + name, shape, dt))


def mm(k, out, lhsT, rhs, start, stop, r, w, tp=None):
    kw = {} if tp is None else {"tile_position": tp}
    k.P.op("tensor", lambda e: e.matmul(out, lhsT, rhs, start=start, stop=stop, **kw), r, w)


def act(k, out, in_, func, r, w, bias=None, scale=None):
    kw = {}
    if bias is not None:
        kw["bias"] = bias
    if scale is not None:
        kw["scale"] = scale
    k.P.op("scalar", lambda e: e.activation(out=out, in_=in_, func=func, **kw), r, w)


def tt(k, out, a, b, op, r, w, eng="vector"):
    k.P.op(eng, lambda e: e.tensor_tensor(out=out, in0=a, in1=b, op=op), r, w)


def ts(k, out, a, s1, op0, r, w, s2=None, op1=None, eng="vector"):
    if op1 is None:
        k.P.op(eng, lambda e: e.tensor_scalar(out=out, in0=a, scalar1=s1, scalar2=None, op0=op0), r, w)
    else:
        k.P.op(eng, lambda e: e.tensor_scalar(out=out, in0=a, scalar1=s1, scalar2=s2, op0=op0, op1=op1), r, w)


def stt(k, out, a, s, b, op0, op1, r, w):
    k.P.op("vector", lambda e: e.scalar_tensor_tensor(out=out, in0=a, scalar=s, in1=b, op0=op0, op1=op1), r, w)


def cp(k, out, in_, r, w, eng="vector"):
    if eng == "scalar":
        k.P.op("scalar", lambda e: e.copy(out=out, in_=in_), r, w)
    else:
        k.P.op(eng, lambda e: e.tensor_copy(out=out, in_=in_), r, w)


def dma(k, out, in_, r, w, q="sync"):
    k.P.op(q, lambda e: e.dma_start(out=out, in_=in_), r, w, dma=True)


def memset(k, ap, val, w, eng="vector"):
    k.P.op(eng, lambda e: e.memset(ap, val), (), w)


def host_consts(core):
    cm = np.zeros((128, 7, 128), np.float32)
    idx = np.arange(128)
    same = (idx[:, None] // 32) == (idx[None, :] // 32)
    cm[:, 0] = np.eye(128)
    cm[:, 1] = 1.0
    cm[:, 2] = ((idx[:, None] // 64) == (idx[None, :] // 64))
    cm[:, 3] = same & (idx[:, None] <= idx[None, :])
    cm[:, 4] = same & (idx[:, None] > idx[None, :])
    cm[:, 5] = same & (idx[:, None] >= idx[None, :])
    cm[:, 6] = same & (idx[:, None] < idx[None, :])
    ind = np.zeros((128, 4), np.float32)
    ind[idx, idx // 32] = 1.0
    R32 = np.zeros((32, 32), np.float32)
    for base in (0, 16):
        for j in range(8):
            R32[base + j, base + 8 + j] = -1.0
            R32[base + 8 + j, base + j] = 1.0
    RT = np.zeros((128, 128), np.float32)
    for g in range(4):
        RT[g * 32:(g + 1) * 32, g * 32:(g + 1) * 32] = R32.T
    n = core * LAT + np.arange(LAT)
    row = (n // 64).astype(np.float32)
    col = (n % 64).astype(np.float32)
    freqs = (10000.0 ** (-np.arange(8, dtype=np.float32) / 8)).astype(np.float32)
    ar = row[None, :] * freqs[:, None]
    ac = col[None, :] * freqs[:, None]
    C32 = np.concatenate([np.cos(ar), np.cos(ar), np.cos(ac), np.cos(ac)], 0)
    S32 = np.concatenate([np.sin(ar), np.sin(ar), np.sin(ac), np.sin(ac)], 0)
    rope = np.zeros((128, 2, RW), np.float32)
    rope[:, 0, :LAT] = np.tile(C32, (4, 1))
    rope[:, 1, :LAT] = np.tile(S32, (4, 1))
    rope[:, 0, LAT:] = 1.0
    hm = np.zeros((128, 2, 8), np.float32)
    hm[:, 0, :] = (np.arange(8) < core)
    hm[:, 1, :] = (np.arange(8) > core)
    return dict(cmat=cm, cind=ind, ropeRT=RT.astype(np.float32), rope=rope.astype(np.float32), hmask=hm)


def declare_layer_inputs(k, mode):
    nc = k.nc
    W = {}

    def di(name, shape, modes="AB", dt=F32):
        if mode in modes:
            W[name] = nc.dram_tensor(name, list(shape), dt, kind="ExternalInput").ap()
    di("cT", [128, 8, 3], "A")
    di("w_ada", [D, 6 * D], "A")
    di("b_adaT", [128, 48], "A")
    di("modT_i", [128, 144], "B")
    di("gn1T", [128, 8])
    di("gn2T", [128, 8])
    di("w_in", [D, INW])
    di("gqT", [128, 2], "B")
    di("w_uq", [256, 384], "B")
    di("gkvT", [128, 1], "A")
    di("w_ukv", [128, 512], "A")
    di("dlam", [128, 128], "B")
    di("gdnT", [128, 1], "B")
    di("ghnT", [128, 1], "B")
    di("hlb", [128, 4, 2, 512])
    di("lmask", [128, 4])
    di("laminit", [128, 2], "B")
    di("w_out", [D, D], "B")
    di("w_g", [D, DFF], "B")
    di("w_u", [D, DFF], "B")
    di("w_d", [DFF, D], "B")
    di("cmat", [128, 7, 128])
    di("cind", [128, 4])
    di("ropeRT", [128, 128])
    di("rope", [128, 2, RW])
    di("hmask", [128, 2, 8], "B")
    return W


def alloc_common(k):
    nc = k.nc
    k.cmat = sbt(k, "cmat", [128, 7, 128], F32)
    k.cmatb = sbt(k, "cmatb", [128, 7, 128], BF16)
    k.cind = sbt(k, "cind", [128, 4], F32)
    k.ropeRT = sbt(k, "ropeRT", [128, 128], F32)
    k.modT = sbt(k, "modT", [128, 48, 3], F32)
    k.G1 = sbt(k, "G1", [128, 8, 3], F32)
    k.G2 = sbt(k, "G2", [128, 8, 3], F32)
    k.gn = sbt(k, "gn", [128, 16], F32)
    k.badaT = sbt(k, "badaT", [128, 48], F32)
    k.cTs = sbt(k, "cTs", [128, 8, 3], F32)
    k.cTb = sbt(k, "cTb", [128, 8, 3], BF16)
    k.wt = [sbt(k, "wt%d" % i, [128, 8, 512], BF16) for i in range(2)]
    k.wt_i = 0
    k.ps = [k.es.enter_context(nc.psum_tensor("ps%d" % i, [128, 512], F32)) for i in range(7)]
    k.psb = k.es.enter_context(nc.psum_tensor("psb", [128, 1024], BF16))
    k.small = sbt(k, "small", [128, 64], F32)
    k.smallw = sbt(k, "smallw", [128, 1408], BF16)


def load_consts(k, W):
    dma(k, k.cmat[:], W["cmat"], (), ["cmat"])
    dma(k, k.cind[:], W["cind"], (), ["cind"])
    dma(k, k.ropeRT[:], W["ropeRT"], (), ["ropeRT"])
    cp(k, k.cmatb[:], k.cmat[:], ["cmat"], ["cmatb"])


def next_wt(k):
    i = k.wt_i
    k.wt_i = 1 - i
    return i


def load_w(k, wsrc, col0, ncols, kc=8):
    i = next_wt(k)
    src = wsrc.rearrange("(kc p) n -> p kc n", p=128)[:, :, col0:col0 + ncols]
    dma(k, k.wt[i][:, 0:kc, 0:ncols], src, (), ["wt%d" % i], q="gpsimd")
    return i


def emit_mod(k, W, compute):
    dma(k, k.gn[:, 0:8], W["gn1T"], (), ["gn"])
    dma(k, k.gn[:, 8:16], W["gn2T"], (), ["gn"])
    if compute:
        dma(k, k.cTs[:], W["cT"], (), ["cTs"])
        dma(k, k.badaT[:], W["b_adaT"], (), ["badaT"])
        act(k, k.cTb[:], k.cTs[:], AF.Silu, ["cTs"], ["cTb"])
        ps = k.ps[0]
        psv = ps[:, 0:144].rearrange("p (c s) -> p c s", s=3)
        for g in range(12):
            wi = load_w(k, W["w_ada"], g * 512, 512)
            for j in range(4):
                oc = g * 4 + j
                for kc in range(8):
                    mm(k, psv[:, oc, :], k.wt[wi][:, kc, j * 128:(j + 1) * 128], k.cTb[:, kc, :], kc == 0, kc == 7,
                       ["wt%d" % wi, "cTb"], ["ps0"])
        tt(k, k.modT[:], psv, k.badaT[:].unsqueeze(2).broadcast_to([128, 48, 3]), ALU.add, ["ps0", "badaT"], ["modT"])
    else:
        dma(k, k.modT[:].rearrange("p c s -> p (c s)"), W["modT_i"], (), ["modT"])
    for (G, name, c0, g0) in ((k.G1, "G1", 8, 0), (k.G2, "G2", 32, 8)):
        ts(k, G[:], k.modT[:, c0:c0 + 8, :], 1.0, ALU.add, ["modT"], [name])
        tt(k, G[:], G[:], k.gn[:, g0:g0 + 8].unsqueeze(2).broadcast_to([128, 8, 3]), ALU.mult, [name, "gn"], [name])


def emit_rstd(k, ps_ap, out_ap, inv_n, r, w):
    act(k, out_ap, ps_ap, AF.Ln, r, w, bias=k.epsb[:ps_ap.shape[0], 0:1], scale=inv_n)
    act(k, out_ap, out_ap, AF.Exp, w, w, scale=-0.5)


def emit_norm_mod(k, xsrc, xkeys, n, G, S, gkeys, out_fn, okeys, sq, sqkey, rstd, rkey):
    ps = k.ps[1]
    act(k, sq[:, :, 0:n], xsrc, AF.Square, xkeys, [sqkey])
    for c in range(8):
        mm(k, ps[:, 0:n], k.cmat[:, 1, :], sq[:, c, 0:n], c == 0, c == 7, ["cmat", sqkey], ["ps1"])
    emit_rstd(k, ps[:, 0:n], rstd[:, 0:n], 1.0 / D, ["ps1"], [rkey])
    for c in range(8):
        if S is None:
            stt(k, out_fn(c), xsrc[:, c, :], G[:, c:c + 1], rstd[:, 0:n], ALU.mult, ALU.mult,
                xkeys + gkeys + [rkey], okeys(c))
        else:
            stt(k, sq[:, c, 0:n], xsrc[:, c, :], G[:, c:c + 1], rstd[:, 0:n], ALU.mult, ALU.mult,
                xkeys + gkeys + [rkey], [sqkey])
            ts(k, out_fn(c), sq[:, c, 0:n], S[:, c:c + 1], ALU.add, [sqkey] + gkeys, okeys(c))


def proj_fm(k, wi, wcol, m, rhs_fn, rkeys, n, kc_n, ps_i, epilogue):
    ps = k.ps[ps_i]
    for kc in range(kc_n):
        mm(k, ps[0:m, 0:n], k.wt[wi][:, kc, wcol:wcol + m], rhs_fn(kc), kc == 0, kc == kc_n - 1,
           ["wt%d" % wi] + rkeys, ["ps%d" % ps_i])
    epilogue(ps[0:m, 0:n], "ps%d" % ps_i)


def emit_rope(k, src_ps, pskey, m, n, tcol0, out_ap, okeys, tmp, tmpkey, tmp2, tmp2key, base=0):
    rows = slice(base, base + m)
    cp(k, tmp[rows, 0:n], src_ps, [pskey], [tmpkey], eng="scalar")
    ps2 = k.ps[6]
    mm(k, ps2[:, 0:n], k.ropeRT[:, :], tmp[:, 0:n], True, True, ["ropeRT", tmpkey], ["ps6"])
    tt(k, tmp2[rows, 0:n], ps2[rows, 0:n], k.rope[rows, 1, tcol0:tcol0 + n], ALU.mult, ["ps6", "rope"], [tmp2key])
    tt(k, tmp[rows, 0:n], tmp[rows, 0:n], k.rope[rows, 0, tcol0:tcol0 + n], ALU.mult, [tmpkey, "rope"], [tmpkey])
    tt(k, out_ap, tmp[rows, 0:n], tmp2[rows, 0:n], ALU.add, [tmpkey, tmp2key], okeys)


def emit_lb(k, W):
    dma(k, k.lbraw[:], W["hlb"], (), ["lbraw"])
    dma(k, k.small[:, 0:4], W["lmask"], (), ["small"])
    act(k, k.lbraw[:], k.lbraw[:], AF.Exp, ["lbraw"], ["lbraw"])
    v = k.lbraw[:].rearrange("p l d f -> p l (d f)")
    tot, part = k.lbt[:, 0, :], k.lbt[:, 1, :]
    tt(k, tot, v[:, 0, :], v[:, 1, :], ALU.add, ["lbraw"], ["lbt"])
    tt(k, tot, tot, v[:, 2, :], ALU.add, ["lbraw", "lbt"], ["lbt"])
    tt(k, tot, tot, v[:, 3, :], ALU.add, ["lbraw", "lbt"], ["lbt"])
    ts(k, part, v[:, 0, :], k.small[:, 0:1], ALU.mult, ["lbraw", "small"], ["lbt"])
    for l in range(1, 4):
        stt(k, part, v[:, l, :], k.small[:, l:l + 1], part, ALU.mult, ALU.add, ["lbraw", "small", "lbt"], ["lbt"])
    k.P.op("vector", lambda e: e.reciprocal(out=tot, in_=tot), ["lbt"], ["lbt"])
    tt(k, part, part, tot, ALU.mult, ["lbt"], ["lbt"])
    ts(k, k.Abc[:].rearrange("p d f -> p (d f)"), part, -1.0, ALU.mult, ["lbt"], ["Abc"], s2=1.0, op1=ALU.add)


class Carver:
    def __init__(self, region, name):
        self.r, self.o, self.name = region, 0, name
        self.n = region.shape[1]

    def take(self, nelem, dt=BF16):
        w = nelem * (2 if dt == F32 else 1)
        w = (w + 1) // 2 * 2
        v = self.r[:, self.o:self.o + w]
        self.o += w
        assert self.o <= self.n, (self.name, self.o, self.n)
        return v.bitcast(F32) if dt == F32 else v


def hgrn_alloc(k, full, cv):
    k.hz = cv.take(512, F32)
    k.hkk = cv.take(512, F32)
    k.hg = cv.take(512, F32)
    k.he = cv.take(512, F32)
    k.hKh = cv.take(512)
    k.hV = cv.take(512)
    if full:
        k.hq = cv.take(512, F32)
        k.hQt = cv.take(512)
        k.hKt = cv.take(512)
        k.hQT = cv.take(512).rearrange("p (f t) -> p f t", f=4)
        k.hKT = [cv.take(512).rearrange("p (f t) -> p f t", f=4) for _ in range(2)]
        k.hAm = cv.take(1024).rearrange("p (h t) -> p h t", h=8)
        k.hVp = [cv.take(512), cv.take(512)]


def hgrn_gates(k, zps, zkey, np_, d, need_full):
    rows = slice(0, np_)
    act(k, k.hz[rows, :], zps, AF.Sigmoid, [zkey], ["hz"], scale=-1.0)
    tt(k, k.hkk[rows, :], k.hz[rows, :], k.Abc[rows, d, :], ALU.mult, ["hz", "Abc"], ["hkk"])
    act(k, k.hg[rows, :], k.hkk[rows, :], AF.Ln, ["hkk"], ["hg"], bias=k.oneb[rows, 0:1], scale=-1.0)
    if getattr(k, "gstop", 0) == 1:
        raise StopBuild()
    psE = k.ps[3]
    mm(k, psE[rows, :], k.cmat[rows, 4 + 2 * d, 0:np_], k.hg[rows, :], True, True, ["cmat", "hg"], ["ps3"])
    act(k, k.he[rows, :], psE[rows, :], AF.Exp, ["ps3"], ["he"])
    tt(k, k.hKh[rows, :], k.hkk[rows, :], k.he[rows, :], ALU.mult, ["hkk", "he"], ["hKh"])
    if getattr(k, "gstop", 0) == 2:
        raise StopBuild()
    psd = k.ps[4]
    nch = np_ // 32
    for fc in range(4):
        if getattr(k, "gstop", 0) == 3:
            continue
        mm(k, psd[:, fc * 4:fc * 4 + 4], k.hg[rows, fc * 128:(fc + 1) * 128], k.cind[rows, 0:4], True, True,
           ["hg", "cind"], ["ps4"])
    if getattr(k, "gstop", 0) == 4:
        raise StopBuild()
    act(k, k.hdec.rearrange("p a b -> p (a b)") if not hasattr(k.hdec, "ap") else k.hdec[:].rearrange("p a b -> p (a b)"), psd[:, 0:16], AF.Exp, ["ps4"], ["hdec"])
    if getattr(k, "gstop", 0) == 3:
        raise StopBuild()


def hgrn_state_step(k, S, skey, F, c, np_rows, with_pad):
    rows = slice(32 * c, 32 * c + 32)
    psS = k.ps[5]
    pv = psS[:, :].rearrange("p (f v) -> p f v", v=128)
    tp = (32 * c, 0)
    for fc in range(4):
        mm(k, pv[:, fc, :], k.hKh[rows, fc * 128:(fc + 1) * 128], k.hV[rows, fc * 128:(fc + 1) * 128], True, True,
           ["hKh", "hV"], ["ps5"], tp=tp)
    dec = k.hdec[:, :, c:c + 1].broadcast_to([128, 4, 64])
    tt(k, S, S, dec, ALU.mult, [skey, "hdec"], [skey])
    tt(k, S[0:64], S[0:64], pv[0:64, :, 0:64], ALU.add, [skey, "ps5"], [skey])
    tt(k, S[64:128], S[64:128], pv[64:128, :, 64:128], ALU.add, [skey, "ps5"], [skey])
    if F is not None:
        tt(k, F, F, k.hdec[:, :, c], ALU.mult, [skey + "F", "hdec"], [skey + "F"])
    if with_pad:
        cp(k, k.Spad[0:64, :, 0:64], S[0:64], [skey], ["Spad"], eng="scalar")
        cp(k, k.Spad[64:128, :, 64:128], S[64:128], [skey], ["Spad"], eng="scalar")


def build_A():
    nc = bass.Bass("TRN2", target_bir_lowering=False)
    k = K()
    k.nc = nc
    k.P = Prog(nc)
    W = declare_layer_inputs(k, "A")
    xT = nc.dram_tensor("xT", [D, NT], F32, kind="ExternalInput").ap()
    modT_o = nc.dram_tensor("modT_o", [128, 144], F32, kind="ExternalOutput").ap()
    exk = nc.dram_tensor("exk", [EXK_SZ], BF16, kind="ExternalOutput").ap()
    exs = nc.dram_tensor("exs", [128, EXS_W], F32, kind="ExternalOutput").ap()
    with ExitStack() as es:
        k.es = es
        alloc_common(k)
        k.rope = sbt(k, "rope", [128, 2, RW], F32)
        k.hdec = sbt(k, "hdec", [128, 4, 4], F32)
        k.Abc = sbt(k, "Abc", [128, 2, 512], F32)
        k.epsb = sbt(k, "epsb", [128, 1], F32)
        k.oneb = sbt(k, "oneb", [128, 1], F32)
        hT = sbt(k, "hT", [128, 8, NT], BF16)
        regA = sbt(k, "regA", [128, 16384], BF16)
        regH = sbt(k, "regH", [128, 5120], BF16)
        hgrn_alloc(k, False, Carver(regH[:], "regH"))
        KTM = sbt(k, "KTM", [64, 4, NT], BF16)
        krr = sbt(k, "krr", [32, NT], BF16)
        ckv = sbt(k, "ckv", [128, 512], F32)
        kvn = sbt(k, "kvn", [128, 512], BF16)
        rstd = sbt(k, "rstd", [128, 512], F32)
        t1 = sbt(k, "t1", [128, 512], F32)
        t2 = sbt(k, "t2", [128, 512], F32)
        Sst = sbt(k, "Sst", [128, 8, 4, 65], F32)
        SKEYS = ["S%d" % c for c in range(8)] + ["S%dF" % c for c in range(8)]
        memset(k, k.epsb[:], EPS, ["epsb"])
        memset(k, k.oneb[:], 1.0, ["oneb"])
        memset(k, t1[:], 0.0, ["t1"])
        memset(k, t2[:], 0.0, ["t2"])
        memset(k, Sst[:, :, :, 0:64], 0.0, SKEYS)
        memset(k, Sst[:, :, :, 64:65], 1.0, SKEYS)
        load_consts(k, W)
        dma(k, k.rope[:], W["rope"], (), ["rope"])
        cvA = Carver(regA[:], "regA0")
        k.lbraw = cvA.take(4096, F32).rearrange("p (l d f) -> p l d f", l=4, d=2)
        k.lbt = cvA.take(2048, F32).rearrange("p (a n) -> p a n", a=2)
        emit_lb(k, W)
        emit_mod(k, W, True)
        dma(k, modT_o, k.modT[:].rearrange("p c s -> p (c s)"), ["modT"], ["modT_o"])
        k.P.barrier()
        cvA = Carver(regA[:], "regA1")
        xb = cvA.take(4096, F32).rearrange("p (c n) -> p c n", c=8)
        sq = cvA.take(4096, F32).rearrange("p (c n) -> p c n", c=8)
        dma(k, k.smallw[:, 768:1280], W["w_ukv"], (), ["smallw"], q="gpsimd")
        dma(k, k.small[:, 8:9], W["gkvT"], (), ["small8"])
        wukv = k.smallw[:, 768:1280]
        xv = xT.rearrange("(c p) n -> p c n", p=128)
        BLOCKS = [(0, 512, 0), (512, 512, 0), (1024, 512, 1), (1536, 512, 1), (2048, 64, 2)]
        for (s0, n, ms) in BLOCKS:
            dma(k, xb[:, :, 0:n], xv[:, :, s0:s0 + n], (), ["xb"])
            emit_norm_mod(k, xb[:, :, 0:n], ["xb"], n, k.G1[:, :, ms], k.modT[:, 0:8, ms], ["G1", "modT"],
                          lambda c, s0=s0, n=n: hT[:, c, s0:s0 + n], lambda c: ["hT"], sq, "sq", rstd, "rstd")

        k.P.barrier()
        cvA = Carver(regA[:], "regA2")
        VM = cvA.take(4 * 17 * VW).rearrange("p (h t v) -> p h t v", h=4, t=17)
        VD = cvA.take(4 * 17 * VW).rearrange("p (h t v) -> p h t v", h=4, t=17)
        KTD = cvA.take(2 * NT).rearrange("p (c n) -> p c n", c=2)
        memset(k, VM, 1.0, ["VM"])
        memset(k, VD, 1.0, ["VD"])

        def tcol(s0):
            return s0 % LAT if s0 < 2 * LAT else LAT
        wi = load_w(k, W["w_in"], O_CKV, 160)
        for (s0, n, ms) in BLOCKS:
            def ep_ckv(ps, pk, s0=s0, n=n):
                cp(k, ckv[:, 0:n], ps, [pk], ["ckv"], eng="scalar")
            proj_fm(k, wi, 0, 128, lambda kc, s0=s0, n=n: hT[:, kc, s0:s0 + n], ["hT"], n, 8, 0, ep_ckv)
            act(k, t1[:, 0:n], ckv[:, 0:n], AF.Square, ["ckv"], ["t1"])
            mm(k, k.ps[1][:, 0:n], k.cmat[:, 1, :], t1[:, 0:n], True, True, ["cmat", "t1"], ["ps1"])
            emit_rstd(k, k.ps[1][:, 0:n], t2[:, 0:n], 1.0 / 128, ["ps1"], ["t2"])
            stt(k, kvn[:, 0:n], ckv[:, 0:n], k.small[:, 8:9], t2[:, 0:n], ALU.mult, ALU.mult,
                ["ckv", "small8", "t2"], ["kvn"])
            for h in range(4):
                mm(k, k.ps[0][0:64, 0:n], wukv[:, h * 128:h * 128 + 64], kvn[:, 0:n], True, True,
                   ["smallw", "kvn"], ["ps0"])
                cp(k, KTM[0:64, h, s0:s0 + n], k.ps[0][0:64, 0:n], ["ps0"], ["KTM"], eng="scalar")
            for t0 in range(0, n, 128):
                tn = min(128, n - t0)
                ti = (s0 + t0) // 128
                pv = k.ps[2][0:tn, 0:256].rearrange("p (h v) -> p h v", v=64)
                mm(k, pv, kvn[:, t0:t0 + tn], wukv.rearrange("p (h x) -> p h x", x=128)[:, :, 64:128], True, True,
                   ["kvn", "smallw"], ["ps2"])
                cp(k, VM[0:tn, :, ti, 0:64], pv, ["ps2"], ["VM"])
            ps = k.ps[0]
            for kc in range(8):
                mm(k, ps[0:32, 0:n], k.wt[wi][:, kc, 128:160], hT[:, kc, s0:s0 + n], kc == 0, kc == 7,
                   ["wt%d" % wi, "hT"], ["ps0"])
            emit_rope(k, ps[0:32, 0:n], "ps0", 32, n, tcol(s0), krr[0:32, s0:s0 + n], ["krr"], t1, "t1", t2, "t2")
        wi = load_w(k, W["w_in"], O_DK, 512)
        for (s0, n, ms) in BLOCKS:
            for c in range(2):
                def ep_dk(ps, pk, s0=s0, n=n, c=c):
                    emit_rope(k, ps, pk, 128, n, tcol(s0), KTD[:, c, s0:s0 + n], ["KTD"], t1, "t1", t2, "t2")
                proj_fm(k, wi, c * 128, 128, lambda kc, s0=s0, n=n: hT[:, kc, s0:s0 + n], ["hT"], n, 8, c, ep_dk)
            for t0 in range(0, n, 128):
                tn = min(128, n - t0)
                ti = (s0 + t0) // 128
                pv = k.ps[2][0:tn, 0:256]
                for kc in range(8):
                    mm(k, pv, hT[:, kc, s0 + t0:s0 + t0 + tn], k.wt[wi][:, kc, 256:512], kc == 0, kc == 7,
                       ["hT", "wt%d" % wi], ["ps2"])
                cp(k, VD[0:tn, :, ti, 0:64], pv.rearrange("p (h v) -> p h v", v=64), ["ps2"], ["VD"])
        ktm_d = exk[OFF_KTM:OFF_KTM + KTM_SZ].rearrange("(h r n) -> r h n", h=4, r=96)
        dma(k, ktm_d[0:64], KTM[:], ["KTM"], ["exk"])
        for h in range(4):
            dma(k, ktm_d[64:96, h, :], krr[:], ["krr"], ["exk"])
        dma(k, exk[OFF_KTD:OFF_KTD + KTD_SZ].rearrange("(c r n) -> r c n", c=2, r=128), KTD, ["KTD"], ["exk"])
        dma(k, exk[OFF_VM:OFF_VM + VM_SZ].rearrange("(h p x) -> p h x", h=4, p=128),
            VM.rearrange("p h t v -> p h (t v)"), ["VM"], ["exk"])
        dma(k, exk[OFF_VD:OFF_VD + VD_SZ].rearrange("(h p x) -> p h x", h=4, p=128),
            VD.rearrange("p h t v -> p h (t v)"), ["VD"], ["exk"])
        TILES = [(t * 128, 128) for t in range(16)] + [(2048, 64)]
        for d in range(2):
            wv = load_w(k, W["w_in"], O_HI, 512)
            wf = load_w(k, W["w_in"], O_HFF if d == 0 else O_HFB, 512)
            order = list(range(17)) if d == 0 else list(range(16, -1, -1))
            for ti in order:
                s0, np_ = TILES[ti]
                rows = slice(0, np_)
                for kc in range(8):
                    mm(k, k.ps[0][rows, :], hT[:, kc, s0:s0 + np_], k.wt[wv][:, kc, :], kc == 0, kc == 7,
                       ["hT", "wt%d" % wv], ["ps0"])
                cp(k, k.hV[rows, :], k.ps[0][rows, :], ["ps0"], ["hV"])
                for kc in range(8):
                    mm(k, k.ps[1][rows, :], hT[:, kc, s0:s0 + np_], k.wt[wf][:, kc, :], kc == 0, kc == 7,
                       ["hT", "wt%d" % wf], ["ps1"])
                hgrn_gates(k, k.ps[1][rows, :], "ps1", np_, d, False)
                nch = np_ // 32
                corder = list(range(nch)) if d == 0 else list(range(nch - 1, -1, -1))
                for c in corder:
                    if ti < 16:
                        chain = (ti // 8) * 2 + d
                    else:
                        chain = 4 + c * 2 + d
                    S = Sst[:, chain, :, 0:64]
                    F = Sst[:, chain, :, 64]
                    hgrn_state_step(k, S, "S%d" % chain, F, c, np_, False)
        dma(k, exs, Sst[:].rearrange("p a f v -> p (a f v)"), SKEYS, ["exs"])
        k.P.finish("sync")
        k.P.emit()
    return nc


def build_B(b, debug=False, stop=99):
    nc = bass.Bass("TRN2", target_bir_lowering=False)
    k = K()
    k.nc = nc
    k.P = Prog(nc)
    P = k.P
    k.gstop = {481: 1, 482: 2, 483: 3, 484: 4}.get(stop, 0)
    W = declare_layer_inputs(k, "B")
    xT = nc.dram_tensor("xT", [D, NT], F32, kind="ExternalInput").ap()
    exk_all = nc.dram_tensor("exk_all", [NCORES, EXK_SZ], BF16, kind="ExternalInput").ap()
    exs_all = nc.dram_tensor("exs_all", [NCORES, 128, EXS_W], F32, kind="ExternalInput").ap()
    xo = nc.dram_tensor("xo", [D, HT], F32, kind="ExternalOutput").ap()
    dbg = nc.dram_tensor("dbg", [D, HT], F32, kind="ExternalOutput").ap() if debug else None
    with ExitStack() as es:
        k.es = es
        alloc_common(k)
        k.hdec = k.small[:, 16:32].rearrange("p (a b) -> p a b", a=4)
        k.Abc = sbt(k, "Abc", [128, 2, 512], F32)
        k.Spad = sbt(k, "Spad", [128, 4, 128], BF16)
        k.mask4 = sbt(k, "mask4", [128, 2, 4, 128], F32)
        k.sel64 = sbt(k, "sel64", [128, 64], F32)
        memset(k, k.sel64[:], 0.0, ["sel64"])
        memset(k, k.sel64[64:65, :], 1.0, ["sel64"])
        k.epsb = sbt(k, "epsb", [128, 1], F32)
        k.oneb = sbt(k, "oneb", [128, 1], F32)
        hB = sbt(k, "hB", [128, 8, HT], BF16)
        mix = sbt(k, "mix", [128, 8, HT], BF16)
        rstd = sbt(k, "rstd", [128, 512], F32)
        t1 = sbt(k, "t1", [128, 512], F32)
        t2 = sbt(k, "t2", [128, 512], F32)
        t3 = sbt(k, "t3", [128, 512], F32)
        ARENA = 50 * 1024
        arena = sbt(k, "arena", [128, ARENA // 2], BF16)
        arena2 = sbt(k, "arena2", [128, 16 * HT], BF16)
        xh = sbt(k, "xh", [128, 8, HT], F32)
        lamt = sbt(k, "lamt", [128, 8], F32)
        Sin = sbt(k, "Sin", [128, 4, 4, 64], F32)
        Sm = sbt(k, "Sm", [128, 4, 64], F32)
        foldm = sbt(k, "foldm", [128, 2, 8], F32)
        memset(k, k.epsb[:], EPS, ["epsb"])
        memset(k, k.oneb[:], 1.0, ["oneb"])
        memset(k, t1[:], 0.0, ["t1"])
        memset(k, t2[:], 0.0, ["t2"])
        load_consts(k, W)
        cv0 = Carver(arena[:], "arena0")
        k.lbraw = cv0.take(4096, F32).rearrange("p (l d f) -> p l d f", l=4, d=2)
        k.lbt = cv0.take(2048, F32).rearrange("p (a n) -> p a n", a=2)
        fold = cv0.take(8 * 4 * 65, F32).rearrange("p (r f v) -> p r f v", r=8, f=4)
        k.smallf = cv0.take(256, F32)
        emit_mod(k, W, False)
        emit_lb(k, W)
        dma(k, foldm[:], W["hmask"], (), ["foldm"])
        dma(k, k.smallw[:, 0:768].rearrange("p (c n) -> p c n", c=2), W["w_uq"].rearrange("(c p) n -> p c n", p=128),
            (), ["smallw"], q="gpsimd")
        dma(k, k.small[:, 8:10], W["gqT"], (), ["small8"])
        dma(k, k.small[:, 10:11], W["gdnT"], (), ["small8"])
        dma(k, k.small[:, 11:12], W["ghnT"], (), ["small8"])
        dma(k, k.small[:, 12:14], W["laminit"], (), ["small8"])
        dma(k, k.smallf[:, 0:128], W["dlam"], (), ["smallf"])
        dl = k.smallf[:, 0:128].rearrange("p (a j) -> p a j", j=32)
        tt(k, k.smallf[:, 128:160], dl[:, 0, :], dl[:, 1, :], ALU.mult, ["smallf"], ["smallf2"])
        tt(k, k.smallf[:, 160:192], dl[:, 2, :], dl[:, 3, :], ALU.mult, ["smallf"], ["smallf2"])
        k.P.op("vector", lambda e: e.tensor_reduce(out=lamt[:, 0:2], in_=k.smallf[:, 128:192].rearrange("p (a j) -> p a j", j=32),
                                                   axis=AX.X, op=ALU.add), ["smallf2"], ["lamt"])
        act(k, lamt[:, 2:4], lamt[:, 0:2], AF.Exp, ["lamt"], ["lamt"])
        tt(k, lamt[:, 4:5], lamt[:, 2:3], lamt[:, 3:4], ALU.subtract, ["lamt"], ["lamt"])
        tt(k, lamt[:, 4:5], lamt[:, 4:5], k.small[:, 12:13], ALU.add, ["lamt", "small8"], ["lamt"])
        ts(k, lamt[:, 5:6], lamt[:, 4:5], -1.0, ALU.mult, ["lamt"], ["lamt"])
        tt(k, lamt[:, 6:7], k.small[:, 10:11], k.small[:, 13:14], ALU.mult, ["small8"], ["lamt"])

        exv = exs_all.rearrange("r p (a x) -> p r a x", a=8)
        for seg in range(2):
            for d in range(2):
                Sx = Sin[:, seg * 2 + d]
                memset(k, Sx, 0.0, ["Sin%d" % (seg * 2 + d)])
                skey = "Sin%d" % (seg * 2 + d)
                steps = []
                order = list(range(8)) if d == 0 else list(range(7, -1, -1))
                if seg == 0:
                    steps += [(4 + b * 2 + d, r, False) for r in order]
                    steps += [(b * 2 + d, r, True) for r in order]
                else:
                    steps += [(4 + b * 2 + d, r, True) for r in order]
                cur = None
                for (chain, r, masked) in steps:
                    if cur != chain:
                        dma(k, fold.rearrange("p r f v -> p r (f v)"), exv[:, :, chain, :], (), ["fold"])
                        cur = chain
                    Sj = fold[:, r, :, 0:64]
                    Fj = fold[:, r, :, 64:65]
                    if masked:
                        m = foldm[:, d, r:r + 1]
                        ts(k, t1[:, 0:4], fold[:, r, :, 64], -1.0, ALU.add, ["fold"], ["t1"], s2=m, op1=ALU.mult)
                        ts(k, t1[:, 0:4], t1[:, 0:4], 1.0, ALU.add, ["t1"], ["t1"])
                        ts(k, t2[:, 0:256].rearrange("p (f v) -> p f v", v=64), Sj, m, ALU.mult, ["fold", "foldm"], ["t2"])
                        tt(k, Sx, Sx, t1[:, 0:4].unsqueeze(2).broadcast_to([128, 4, 64]), ALU.mult, [skey, "t1"], [skey])
                        tt(k, Sx, Sx, t2[:, 0:256].rearrange("p (f v) -> p f v", v=64), ALU.add, [skey, "t2"], [skey])
                    else:
                        tt(k, Sx, Sx, Fj.broadcast_to([128, 4, 64]), ALU.mult, [skey, "fold"], [skey])
                        tt(k, Sx, Sx, Sj, ALU.add, [skey, "fold"], [skey])

        if stop == 0:
            P.barrier()
            dma(k, xo.rearrange("(c p) n -> p c n", p=128), xh[:], [], ["xo"])
            P.finish("sync")
            P.emit()
            return nc
        xv = xT.rearrange("(c p) n -> p c n", p=128)

        def load_xh():
            dma(k, xh[:, :, 0:LAT], xv[:, :, b * LAT:(b + 1) * LAT], (), ["xh"])
            dma(k, xh[:, :, LAT:HT], xv[:, :, 2 * LAT + b * CT:2 * LAT + (b + 1) * CT], (), ["xh"])
        load_xh()
        HBLK = [(0, 512, b), (512, 512, b), (1024, 32, 2)]
        P.barrier()
        sq = arena[:, 0:8192].bitcast(F32).rearrange("p (c n) -> p c n", c=8)
        for (s0, n, ms) in HBLK:
            emit_norm_mod(k, xh[:, :, s0:s0 + n], ["xh"], n, k.G1[:, :, ms], k.modT[:, 0:8, ms], ["G1", "modT"],
                          lambda c, s0=s0, n=n: hB[:, c, s0:s0 + n], lambda c: ["hB"], sq, "arena", rstd, "rstd")
        P.barrier()
        cv2 = Carver(arena2[:], "arena2a")
        k.rope = cv2.take(2 * RW, F32).rearrange("p (a n) -> p a n", a=2)
        dma(k, k.rope, W["rope"], (), ["rope"])
        if stop == 1:
            P.barrier()
            dma(k, xo.rearrange("(c p) n -> p c n", p=128), xh[:], [], ["xo"])
            P.finish("sync")
            P.emit()
            return nc
        qm = arena[0:96, 0:4 * HT].rearrange("p (h n) -> p h n", h=4)
        qd = arena[:, 4 * HT:6 * HT].rearrange("p (c n) -> p c n", c=2)
        o0 = 6 * HT
        cq = cv2.take(1024, F32).rearrange("p (c n) -> p c n", c=2)
        cqn = cv2.take(1024).rearrange("p (c n) -> p c n", c=2)
        memset(k, arena[64:128, 0:4 * HT], 0.0, ["qm"])
        wi = load_w(k, W["w_in"], O_CQ, 256)
        wuq = k.smallw[:, 0:768].rearrange("p (c n) -> p c n", c=2)
        for (s0, n, ms) in HBLK:
            for c in range(2):
                def ep_cq(ps, pk, c=c, n=n):
                    cp(k, cq[:, c, 0:n], ps, [pk], ["cq"], eng="scalar")
                proj_fm(k, wi, c * 128, 128, lambda kc, s0=s0, n=n: hB[:, kc, s0:s0 + n], ["hB"], n, 8, c, ep_cq)
            act(k, t1[:, 0:n], cq[:, 0, 0:n], AF.Square, ["cq"], ["t1"])
            act(k, t2[:, 0:n], cq[:, 1, 0:n], AF.Square, ["cq"], ["t2"])
            mm(k, k.ps[2][:, 0:n], k.cmat[:, 1, :], t1[:, 0:n], True, False, ["cmat", "t1"], ["ps2"])
            mm(k, k.ps[2][:, 0:n], k.cmat[:, 1, :], t2[:, 0:n], False, True, ["cmat", "t2"], ["ps2"])
            emit_rstd(k, k.ps[2][:, 0:n], t3[:, 0:n], 1.0 / 256, ["ps2"], ["t3"])
            for c in range(2):
                stt(k, cqn[:, c, 0:n], cq[:, c, 0:n], k.small[:, 8 + c:9 + c], t3[:, 0:n], ALU.mult, ALU.mult,
                    ["cq", "small8", "t3"], ["cqn"])
            tc0 = s0 if s0 < LAT else LAT
            for h in range(4):
                ps = k.ps[3]
                for c in range(2):
                    mm(k, ps[0:96, 0:n], wuq[:, c, h * 96:(h + 1) * 96], cqn[:, c, 0:n], c == 0, c == 1,
                       ["smallw", "cqn"], ["ps3"])
                cp(k, qm[0:64, h, s0:s0 + n], ps[0:64, 0:n], ["ps3"], ["qm"], eng="scalar")
                emit_rope(k, ps[64:96, 0:n], "ps3", 32, n, tc0, qm[64:96, h, s0:s0 + n], ["qm"], t1, "t1", t2, "t2", base=64)
        wi = load_w(k, W["w_in"], O_DQ, 256)
        for (s0, n, ms) in HBLK:
            tc0 = s0 if s0 < LAT else LAT
            for c in range(2):
                def ep_dq(ps, pk, c=c, s0=s0, n=n, tc0=tc0):
                    emit_rope(k, ps, pk, 128, n, tc0, qd[:, c, s0:s0 + n], ["qd"], t1, "t1", t2, "t2")
                proj_fm(k, wi, c * 128, 128, lambda kc, s0=s0, n=n: hB[:, kc, s0:s0 + n], ["hB"], n, 8, c, ep_dq)

        if stop == 2:
            P.barrier()
            if debug:
                dv = dbg.rearrange("(c p) n -> p c n", p=128)
                for h in range(4):
                    cp(k, t1[0:96, 0:512], qm[0:96, h, 0:512], [], ["t1"])
                    dma(k, dv[0:96, h, 0:512], t1[0:96, 0:512], ["t1"], ["dbg"])
                for c in range(2):
                    cp(k, t1[:, 0:512], qd[:, c, 0:512], [], ["t1"])
                    dma(k, dv[:, 4 + c, 0:512], t1[:, 0:512], ["t1"], ["dbg"])
                P.barrier()
            dma(k, xo.rearrange("(c p) n -> p c n", p=128), xh[:], [], ["xo"])
            P.finish("sync")
            P.emit()
            return nc
        KTs = [arena[:, o0 + i * 4352:o0 + (i + 1) * 4352] for i in range(2)]
        o0 += 2 * 4352
        Vs = [arena[:, o0 + i * 34 * VW:o0 + (i + 1) * 34 * VW].rearrange("p (t v) -> p t v", v=VW) for i in range(2)]
        o0 += 2 * 34 * VW
        KTc = arena[:, o0:o0 + 1024].rearrange("p (h n) -> p h n", h=4)
        o0 += 1024
        Vc = arena[:, o0:o0 + 8 * VW].rearrange("p (u h v) -> p u h v", u=2, h=4)
        o0 += 8 * VW
        pts = [cv2.take(512) for i in range(4)]
        osb = cv2.take(512, F32)
        osb2 = cv2.take(512, F32)
        dtmp = cv2.take(HT)
        assert o0 * 2 <= ARENA, o0 * 2
        exk3 = exk_all

        def kv_views(kind, h):
            if kind == "mla":
                kt = exk3[:, OFF_KTM:OFF_KTM + KTM_SZ].rearrange("r (h q n) -> r h q n", h=4, q=96)
                vv = exk3[:, OFF_VM:OFF_VM + VM_SZ].rearrange("r (h p t v) -> r h p t v", h=4, p=128, t=17)
                rows = slice(0, 96)
                ktsrc = lambda hf: kt[4 * hf:4 * hf + 4, h, :, b * LAT:(b + 1) * LAT].rearrange("r q n -> q r n")
                ktc = lambda r: kt[r, :, :, 2 * LAT + b * CT:2 * LAT + (b + 1) * CT].rearrange("h q n -> q h n")
            else:
                kt = exk3[:, OFF_KTD:OFF_KTD + KTD_SZ].rearrange("r (c q n) -> r c q n", c=2, q=128)
                vv = exk3[:, OFF_VD:OFF_VD + VD_SZ].rearrange("r (h p t v) -> r h p t v", h=4, p=128, t=17)
                rows = slice((h % 2) * 64, (h % 2) * 64 + 64)
                ktsrc = lambda hf: kt[4 * hf:4 * hf + 4, h // 2, rows, b * LAT:(b + 1) * LAT].rearrange("r q n -> q r n")
                ktc = lambda r: kt[r, :, :, 2 * LAT + b * CT:2 * LAT + (b + 1) * CT].rearrange("c q n -> q c n")
            vsrc = lambda hf: vv[4 * hf:4 * hf + 4, h, :, 8 * b:8 * b + 8, :].rearrange("r p t v -> p r (t v)")
            vc = lambda r: vv[r, :, 32 * b:32 * b + 32, 16, :].rearrange("h p v -> p h v")
            return ktsrc, rows, vsrc, ktc, vc

        def attention(kind):
            nmap = 1 if kind == "mla" else 2
            scale = MLA_SCALE if kind == "mla" else DIFF_SCALE
            _, _, _, ktc, vc = kv_views(kind, 0)
            nh_c = 4 if kind == "mla" else 2
            rows_c = slice(0, 96) if kind == "mla" else slice(0, 128)
            for r in range(8):
                dma(k, KTc[rows_c, 0:nh_c, r * 32:(r + 1) * 32], ktc(r), (), ["KTc"])
                dma(k, Vc[(r % 4) * 32:(r % 4) * 32 + 32, r // 4, :, :], vc(r), (), ["Vc"])
            units = [(h, qb, hf) for h in range(4) for qb in range(2) for hf in range(2)]
            loaded = {}

            def load_unit(u):
                h, qb, hf = units[u]
                ktsrc, rows, vsrc, _, _ = kv_views(kind, h)
                sl = u % 2
                dma(k, KTs[sl][rows, 0:4096].rearrange("p (r n) -> p r n", r=4), ktsrc(hf), (), ["KT%d" % sl])
                dma(k, Vs[sl][:, 0:32, :].rearrange("p (r t) v -> p r (t v)", r=4), vsrc(hf), (), ["V%d" % sl])
                loaded[u] = sl
            load_unit(0)
            pti = [0]
            psi = [0]

            def key_tile(h, qcols, nq, kt_ap_fn, v_ap, kkeys, accs, first, last):
                for m in range(nmap):
                    if kind == "mla":
                        base, kk = 0, 96
                        qrhs = qm[0:96, h, qcols]
                        tp = None
                    else:
                        base, kk = (h % 2) * 64 + m * 32, 32
                        qrhs = qd[base:base + 32, h // 2, qcols]
                        tp = (base, 0)
                    pi = 2 + psi[0] % 3
                    psi[0] += 1
                    ps = k.ps[pi]
                    mm(k, ps[:, 0:nq], kt_ap_fn(base, kk), qrhs, True, True, kkeys + ["qm", "qd"], ["ps%d" % pi], tp=tp)
                    pj = pti[0] % 4
                    pti[0] += 1
                    act(k, pts[pj][:, 0:nq], ps[:, 0:nq], AF.Exp, ["ps%d" % pi], ["pt%d" % pj], scale=scale)
                    ai = accs[m]
                    mm(k, k.ps[ai][0:65, 0:nq], v_ap, pts[pj][:, 0:nq], first, last, kkeys + ["pt%d" % pj], ["ps%d" % ai])

            def epilogue(h, cols0, nq, accs):
                n = nq
                outs = []
                for m in range(nmap):
                    ai = accs[m]
                    pa = k.ps[ai]
                    act(k, t1[64:65, 0:n], pa[64:65, 0:n], AF.Ln, ["ps%d" % ai], ["t1"])
                    act(k, t1[64:65, 0:n], t1[64:65, 0:n], AF.Exp, ["t1"], ["t1"], scale=-1.0)
                    cp(k, t2[64:65, 0:n], t1[64:65, 0:n], ["t1"], ["t2"])
                    dst = osb if m == 0 else osb2
                    cp(k, dst[0:64, 0:n], pa[0:64, 0:n], ["ps%d" % ai], ["osb%d" % m], eng="scalar")
                    mm(k, k.ps[1][0:64, 0:n], k.sel64[:, :], t2[:, 0:n], True, True, ["sel64", "t2"], ["ps1"])
                    if debug and kind == "mla" and h == 0 and cols0 == 0 and stop == 3:
                        dv = dbg.rearrange("(c p) n -> p c n", p=128)
                        cp(k, t3[0:65, 0:n], pa[0:65, 0:n], ["ps%d" % ai], ["t3"])
                        dma(k, dv[0:65, 0, 0:n], t3[0:65, 0:n], ["t3"], ["dbg"])
                        dma(k, dv[0:64, 1, 0:n], dst[0:64, 0:n], ["osb%d" % m], ["dbg"])
                        cp(k, t3[0:64, 0:n], k.ps[1][0:64, 0:n], ["ps1", "dbg"], ["t3"])
                        dma(k, dv[0:64, 2, 0:n], t3[0:64, 0:n], ["t3"], ["dbg"])
                        dma(k, dv[:, 3, 0:n], t2[:, 0:n], ["t2"], ["dbg"])
                    tt(k, dst[0:64, 0:n], dst[0:64, 0:n], k.ps[1][0:64, 0:n], ALU.mult, ["osb%d" % m, "ps1"], ["osb%d" % m])
                if kind == "mla":
                    feat0 = h * 64
                    src = osb
                    skey = ["osb0"]
                else:
                    stt(k, osb[0:64, 0:n], osb2[0:64, 0:n], lamt[0:64, 5:6], osb[0:64, 0:n], ALU.mult, ALU.add,
                        ["osb0", "osb1", "lamt"], ["osb0"])
                    act(k, osb2[0:64, 0:n], osb[0:64, 0:n], AF.Square, ["osb0"], ["osb1"])
                    mm(k, k.ps[1][0:64, 0:n], k.cmat[0:64, 1, 0:64], osb2[0:64, 0:n], True, True, ["cmat", "osb1"], ["ps1"])
                    emit_rstd(k, k.ps[1][0:64, 0:n], osb2[0:64, 0:n], 1.0 / 64, ["ps1"], ["osb1"])
                    stt(k, osb[0:64, 0:n], osb[0:64, 0:n], lamt[0:64, 6:7], osb2[0:64, 0:n], ALU.mult, ALU.mult,
                        ["osb0", "osb1", "lamt"], ["osb0"])
                    feat0 = 256 + h * 64
                    src = osb
                    skey = ["osb0"]
                ch, r0 = feat0 // 128, feat0 % 128
                if r0 == 0:
                    cp(k, mix[0:64, ch, cols0:cols0 + n], src[0:64, 0:n], skey, ["mix"])
                else:
                    cp(k, dtmp[0:64, cols0:cols0 + n], src[0:64, 0:n], skey, ["dtmp"])

            for u in range(len(units)):
                h, qb, hf = units[u]
                sl = loaded[u]
                if u + 1 < len(units):
                    load_unit(u + 1)
                accs = [5, 6]
                qcols = slice(qb * 512, (qb + 1) * 512)
                ntile = 32
                for kt in range(ntile):
                    first = (hf == 0 and kt == 0)
                    last = False
                    key_tile(h, qcols, 512,
                             lambda base, kk, sl=sl, kt=kt: KTs[sl][base:base + kk, kt * 128:(kt + 1) * 128],
                             Vs[sl][:, kt, 0:65], ["KT%d" % sl, "V%d" % sl], accs, first, last)
                if hf == 1:
                    for u2 in range(2):
                        if kind == "mla":
                            ktf = lambda base, kk, u2=u2: KTc[base:base + kk, h, u2 * 128:(u2 + 1) * 128]
                        else:
                            ktf = lambda base, kk, u2=u2: KTc[base:base + kk, h // 2, u2 * 128:(u2 + 1) * 128]
                        key_tile(h, qcols, 512, ktf, Vc[:, u2, h, 0:65], ["KTc", "Vc"], accs, False, u2 == 1)
                    epilogue(h, qb * 512, 512, accs)
                    if qb == 1:
                        for u2 in range(2):
                            if kind == "mla":
                                ktf = lambda base, kk, u2=u2: KTc[base:base + kk, h, u2 * 128:(u2 + 1) * 128]
                            else:
                                ktf = lambda base, kk, u2=u2: KTc[base:base + kk, h // 2, u2 * 128:(u2 + 1) * 128]
                            key_tile(h, slice(LAT, HT), CT, ktf, Vc[:, u2, h, 0:65], ["KTc", "Vc"], accs, u2 == 0, u2 == 1)
                        epilogue(h, LAT, CT, accs)
                        feat0 = (0 if kind == "mla" else 256) + h * 64
                        if feat0 % 128 != 0:
                            dma(k, mix[64:128, feat0 // 128, :], dtmp[0:64, :], ["dtmp"], ["mix"])

        if stop == 3:
            attention("mla")
            P.barrier()
            dma(k, xo.rearrange("(c p) n -> p c n", p=128), xh[:], [], ["xo"])
            P.finish("sync")
            P.emit()
            return nc
        if not (45 <= stop <= 60 or 480 <= stop <= 510) or debug:
            memset(k, KTs[0][64:128, :], 0.0, ["KT0"])
            memset(k, KTs[1][64:128, :], 0.0, ["KT1"])
            attention("mla")
            attention("diff")
        if stop == 4:
            P.barrier()
            if debug:
                dbgf = arena[:, 0:2 * HT].bitcast(F32)
                for c in range(4):
                    cp(k, dbgf[:, 0:HT], mix[:, c, :], ["mix"], ["arena"])
                    dma(k, dbg.rearrange("(c p) n -> p c n", p=128)[:, c, :], dbgf[:, 0:HT], ["arena"], ["dbg"])
                P.barrier()
            dma(k, xo.rearrange("(c p) n -> p c n", p=128), xh[:], [], ["xo"])
            P.finish("sync")
            P.emit()
            return nc
        P.barrier()

        o0 = 0
        oacc = arena[:, o0:o0 + 2 * 4 * HT].bitcast(F32).rearrange("p (f n) -> p f n", f=4)
        o0 += 2 * 4 * HT
        hw = [arena[:, o0 + i * 4096:o0 + (i + 1) * 4096].rearrange("p (c n) -> p c n", c=8) for i in range(4)]
        o0 += 4 * 4096
        assert o0 * 2 <= ARENA, o0 * 2
        for i, off in enumerate((O_HQ, O_HI, O_HFF, O_HFB)):
            src = W["w_in"].rearrange("(kc p) n -> p kc n", p=128)[:, :, off:off + 512]
            dma(k, hw[i][:], src, (), ["hw%d" % i], q="gpsimd")
        memset(k, k.Spad[:], 0.0, ["Spad"])
        hgrn_alloc(k, True, Carver(arena2[:], "arena2h"))
        for d_ in range(2):
            for j_ in range(4):
                cp(k, k.mask4[:, d_, j_, :], k.cmat[:, 3 + 2 * d_, :], ["cmat"], ["mask4"])
        memset(k, k.hVp[0], 0.0, ["hVp0"])
        memset(k, k.hVp[1], 0.0, ["hVp1"])
        TL = [(t * 128, 128) for t in range(8)] + [(LAT, 32)]
        try:
          for d in range(2):
              for seg in (0, 1):
                  tiles = list(range(8)) if seg == 0 else [8]
                  if d == 1:
                      tiles = tiles[::-1]
                  cp(k, Sm[:], Sin[:, seg * 2 + d], ["Sin%d" % (seg * 2 + d)], ["Sm"])
                  cp(k, k.Spad[0:64, :, 0:64], Sm[0:64], ["Sm"], ["Spad"], eng="scalar")
                  cp(k, k.Spad[64:128, :, 64:128], Sm[64:128], ["Sm"], ["Spad"], eng="scalar")
                  if stop == 49:
                      raise StopBuild()
                  for ti in tiles:
                      s0, np_ = TL[ti]
                      rows = slice(0, np_)
                      for kc in range(8):
                          mm(k, k.ps[0][rows, :], hB[:, kc, s0:s0 + np_], hw[0][:, kc, :], kc == 0, kc == 7,
                             ["hB", "hw0"], ["ps0"])
                      if stop == 46:
                          raise StopBuild()
                      act(k, k.hq[rows, :], k.ps[0][rows, :], AF.Silu, ["ps0"], ["hq"])
                      if stop == 47:
                          raise StopBuild()
                      for kc in range(8):
                          mm(k, k.ps[1][rows, :], hB[:, kc, s0:s0 + np_], hw[1][:, kc, :], kc == 0, kc == 7,
                             ["hB", "hw1"], ["ps1"])
                      cp(k, k.hV[rows, :], k.ps[1][rows, :], ["ps1"], ["hV"])
                      for kc in range(8):
                          mm(k, k.ps[2][rows, :], hB[:, kc, s0:s0 + np_], hw[2 + d][:, kc, :], kc == 0, kc == 7,
                             ["hB", "hw%d" % (2 + d)], ["ps2"])
                      if stop == 48:
                          raise StopBuild()
                      hgrn_gates(k, k.ps[2][rows, :], "ps2", np_, d, True)
                      if stop == 50:
                          raise StopBuild()
                      mm(k, k.ps[3][rows, :], k.cmat[rows, 3 + 2 * d, 0:np_], k.hg[rows, :], True, True, ["cmat", "hg"], ["ps3"])
                      act(k, k.he[rows, :], k.ps[3][rows, :], AF.Exp, ["ps3"], ["he"])
                      tt(k, k.hQt[rows, :], k.hq[rows, :], k.he[rows, :], ALU.mult, ["hq", "he"], ["hQt"])
                      act(k, k.he[rows, :], k.ps[3][rows, :], AF.Exp, ["ps3"], ["he"], scale=-1.0)
                      tt(k, k.hKt[rows, :], k.hkk[rows, :], k.he[rows, :], ALU.mult, ["hkk", "he"], ["hKt"])
                      if stop == 505:
                          raise StopBuild()
                      pq = k.ps[0][:, :].rearrange("p (f t) -> p f t", f=4)
                      pk_ = k.ps[1][:, :].rearrange("p (f t) -> p f t", f=4)
                      for fc in range(4):
                          mm(k, pq[:, fc, 0:np_], k.hQt[rows, fc * 128:(fc + 1) * 128], k.cmatb[rows, 0, 0:np_], True, True,
                             ["hQt", "cmatb"], ["ps0"])
                      for fc in range(4):
                          mm(k, pk_[:, fc, 0:np_], k.hKt[rows, fc * 128:(fc + 1) * 128], k.cmatb[rows, 0, 0:np_], True, True,
                             ["hKt", "cmatb"], ["ps1"])
                      if stop == 506:
                          raise StopBuild()
                      cp(k, k.hQT[:, :, 0:np_], pq[:, :, 0:np_], ["ps0"], ["hQT"], eng="scalar")
                      for a in range(2):
                          ts(k, k.hKT[a][:, :, 0:np_], pk_[:, :, 0:np_], k.cmat[:, 2, 64 * a:64 * a + 1], ALU.mult,
                             ["ps1", "cmat"], ["hKT%d" % a])
                      if stop == 51:
                          raise StopBuild()
                      pA = [k.ps[0], k.ps[1]]
                      for hh in range(8):
                          fc, a = hh // 2, hh % 2
                          pa = pA[hh // 4][rows, (hh % 4) * 128:(hh % 4) * 128 + np_]
                          mm(k, pa, k.hKT[a][:, fc, 0:np_], k.hQT[:, fc, 0:np_], True, True,
                             ["hKT%d" % a, "hQT"], ["ps%d" % (hh // 4)])
                      msk = k.mask4[rows, d, :, 0:np_]
                      for half in range(2):
                          tt(k, k.hAm[rows, half * 4:half * 4 + 4, 0:np_],
                             pA[half][rows, :].rearrange("p (h t) -> p h t", h=4)[:, :, 0:np_], msk, ALU.mult,
                             ["ps%d" % half, "mask4"], ["hAm"])
                      if stop == 52:
                          raise StopBuild()
                      po = k.ps[6][:, :].rearrange("p (f t) -> p f t", f=4)
                      firstmm = True
                      for fc in range(4):
                          for a in range(2):
                              pass
                      for a in range(2):
                          vp = k.hVp[a]
                          cp(k, vp[rows, :].rearrange("p (f x) -> p f x", x=128)[:, :, 64 * a:64 * a + 64],
                             k.hV[rows, :].rearrange("p (f x) -> p f x", x=128)[:, :, 64 * a:64 * a + 64], ["hV"], ["hVp%d" % a],
                             eng=("vector" if a == 0 else "scalar"))
                      for fc in range(4):
                          for a in range(2):
                              mm(k, po[:, fc, 0:np_], k.hVp[a][rows, fc * 128:(fc + 1) * 128], k.hAm[rows, fc * 2 + a, 0:np_],
                                 firstmm, False, ["hVp%d" % a, "hAm"], ["ps6"])
                              firstmm = False
                      nch = np_ // 32
                      corder = list(range(nch)) if d == 0 else list(range(nch - 1, -1, -1))
                      for ci, c in enumerate(corder):
                          for fc in range(4):
                              mm(k, po[:, fc, 32 * c:32 * c + 32], k.Spad[:, fc, :], k.hQT[:, fc, 32 * c:32 * c + 32],
                                 False, (ci == nch - 1 and fc == 3), ["Spad", "hQT"], ["ps6"])
                          hgrn_state_step(k, Sm[:], "Sm", None, c, np_, True)
                      if stop == 53:
                          raise StopBuild()
                      if d == 0:
                          cp(k, oacc[:, :, s0:s0 + np_], po[:, :, 0:np_], ["ps6"], ["oacc"])
                      else:
                          tt(k, oacc[:, :, s0:s0 + np_], oacc[:, :, s0:s0 + np_], po[:, :, 0:np_], ALU.add, ["ps6", "oacc"], ["oacc"])
        except StopBuild:
            stop = 5
        if stop == 5:
            P.barrier()
            if debug:
                dbgf = t1
                for c in range(4):
                    for hh in range(2):
                        cp(k, dbgf[:, 0:512], mix[:, c, hh * 512:(hh + 1) * 512], ["mix"], ["t1"])
                        dma(k, dbg.rearrange("(c p) n -> p c n", p=128)[:, c, hh * 512:(hh + 1) * 512], dbgf[:, 0:512], ["t1"], ["dbg"])
                P.barrier()
            dma(k, xo.rearrange("(c p) n -> p c n", p=128), xh[:], [], ["xo"])
            P.finish("sync")
            P.emit()
            return nc
        wi = load_w(k, W["w_in"], O_HG, 512)
        for (s0, n, ms) in HBLK:
            for fc in range(4):
                act(k, t1[:, 0:n], oacc[:, fc, s0:s0 + n], AF.Square, ["oacc"], ["t1"])
                mm(k, k.ps[2][:, 0:n], k.cmat[:, 2, :], t1[:, 0:n], True, True, ["cmat", "t1"], ["ps2"])
                emit_rstd(k, k.ps[2][:, 0:n], t2[:, 0:n], 1.0 / 64, ["ps2"], ["t2"])
                stt(k, t2[:, 0:n], oacc[:, fc, s0:s0 + n], k.small[:, 11:12], t2[:, 0:n], ALU.mult, ALU.mult,
                    ["oacc", "small8", "t2"], ["t2"])

                def ep_g(ps, pk, fc=fc, s0=s0, n=n):
                    act(k, t3[:, 0:n], ps, AF.Silu, [pk], ["t3"])
                    tt(k, mix[:, 4 + fc, s0:s0 + n], t2[:, 0:n], t3[:, 0:n], ALU.mult, ["t2", "t3"], ["mix"])
                proj_fm(k, wi, fc * 128, 128, lambda kc, s0=s0, n=n: hB[:, kc, s0:s0 + n], ["hB"], n, 8, fc % 2, ep_g)
        P.barrier()
        if debug:
            for c in range(8):
                for hh in range(2):
                    cp(k, t1[:, 0:512], mix[:, c, hh * 512:(hh + 1) * 512], ["mix"], ["t1"])
                    dma(k, dbg.rearrange("(c p) n -> p c n", p=128)[:, c, hh * 512:(hh + 1) * 512], t1[:, 0:512], ["t1"], ["dbg"])
            P.barrier()

        for g in range(2):
            wi = load_w(k, W["w_out"], g * 512, 512)
            for j in range(4):
                oc = g * 4 + j
                for (s0, n, ms) in HBLK:
                    def ep_o(ps, pk, oc=oc, s0=s0, n=n, ms=ms):
                        stt(k, xh[:, oc, s0:s0 + n], ps, k.modT[:, 16 + oc, ms:ms + 1], xh[:, oc, s0:s0 + n], ALU.mult, ALU.add,
                            [pk, "modT", "xh"], ["xh"])
                    proj_fm(k, wi, j * 128, 128, lambda kc, s0=s0, n=n: mix[:, kc, s0:s0 + n], ["mix"], n, 8, oc % 2, ep_o)
        if stop == 6:
            P.barrier()
            dma(k, xo.rearrange("(c p) n -> p c n", p=128), xh[:], [], ["xo"])
            P.finish("sync")
            P.emit()
            return nc
        P.barrier()
        sq = arena[:, 0:8192].bitcast(F32).rearrange("p (c n) -> p c n", c=8)
        for (s0, n, ms) in HBLK:
            emit_norm_mod(k, xh[:, :, s0:s0 + n], ["xh"], n, k.G2[:, :, ms], k.modT[:, 24:32, ms], ["G2", "modT"],
                          lambda c, s0=s0, n=n: hB[:, c, s0:s0 + n], lambda c: ["hB"], sq, "arena", rstd, "rstd")
        P.barrier()
        actT = arena[:, 0:NFF * HT].rearrange("p (f n) -> p f n", f=NFF)
        for g in range(11):
            i = next_wt(k)
            for part, wsrc in ((0, W["w_g"]), (1, W["w_u"])):
                src = wsrc.rearrange("(kc p) n -> p kc n", p=128)[:, :, g * 256:(g + 1) * 256]
                dma(k, k.wt[i][:, :, part * 256:(part + 1) * 256], src, (), ["wt%d" % i], q="gpsimd")
            for j in range(2):
                f = g * 2 + j
                for (s0, n, ms) in HBLK:
                    psg, psu = k.ps[0 + 2 * (j % 2)], k.ps[1 + 2 * (j % 2)]
                    kg, ku = "ps%d" % (2 * (j % 2)), "ps%d" % (1 + 2 * (j % 2))
                    for kc in range(8):
                        mm(k, psg[:, 0:n], k.wt[i][:, kc, j * 128:(j + 1) * 128], hB[:, kc, s0:s0 + n], kc == 0, kc == 7,
                           ["wt%d" % i, "hB"], [kg])
                    for kc in range(8):
                        mm(k, psu[:, 0:n], k.wt[i][:, kc, 256 + j * 128:256 + (j + 1) * 128], hB[:, kc, s0:s0 + n], kc == 0, kc == 7,
                           ["wt%d" % i, "hB"], [ku])
                    act(k, t1[:, 0:n], psg[:, 0:n], AF.Silu, [kg], ["t1"])
                    tt(k, actT[:, f, s0:s0 + n], t1[:, 0:n], psu[:, 0:n], ALU.mult, ["t1", ku], ["actT"])
        wdv = W["w_d"].rearrange("(f p) n -> p f n", p=128)
        wd = [k.wt[0][:].rearrange("p a b -> p (a b)")[:, 0:NFF * 128].rearrange("p (f n) -> p f n", f=NFF),
              k.wt[1][:].rearrange("p a b -> p (a b)")[:, 0:NFF * 128].rearrange("p (f n) -> p f n", f=NFF)]
        for oc in range(8):
            i = next_wt(k)
            dma(k, wd[i], wdv[:, :, oc * 128:(oc + 1) * 128], (), ["wt%d" % i], q="gpsimd")
            for (s0, n, ms) in HBLK:
                ps = k.ps[oc % 2]
                for f in range(NFF):
                    mm(k, ps[:, 0:n], wd[i][:, f, :], actT[:, f, s0:s0 + n], f == 0, f == NFF - 1,
                       ["wt%d" % i, "actT"], ["ps%d" % (oc % 2)])
                stt(k, xh[:, oc, s0:s0 + n], ps[:, 0:n], k.modT[:, 40 + oc, ms:ms + 1], xh[:, oc, s0:s0 + n], ALU.mult, ALU.add,
                    ["ps%d" % (oc % 2), "modT", "xh"], ["xh"])
        dma(k, xo.rearrange("(c p) n -> p c n", p=128), xh[:], ["xh"], ["xo"])
        P.finish("sync")
        P.emit()
    return nc


def build_F():
    nc = bass.Bass("TRN2", target_bir_lowering=False)
    k = K()
    k.nc = nc
    k.P = Prog(nc)
    xT = nc.dram_tensor("xT", [D, NT], F32, kind="ExternalInput").ap()
    gfT = nc.dram_tensor("gfT", [128, 8], F32, kind="ExternalInput").ap()
    cmat_d = nc.dram_tensor("cmat", [128, 7, 128], F32, kind="ExternalInput").ap()
    yT = nc.dram_tensor("yT", [D, 2 * LAT], F32, kind="ExternalOutput").ap()
    with ExitStack() as es:
        k.es = es
        k.cmat = sbt(k, "cmat", [128, 7, 128], F32)
        k.ps = [k.es.enter_context(nc.psum_tensor("ps%d" % i, [128, 512], F32)) for i in range(2)]
        k.epsb = sbt(k, "epsb", [128, 1], F32)
        gf = sbt(k, "gf", [128, 8], F32)
        xb = [sbt(k, "xb%d" % i, [128, 8, 512], F32) for i in range(2)]
        yb = [sbt(k, "yb%d" % i, [128, 8, 512], F32) for i in range(2)]
        sq = sbt(k, "sq", [128, 8, 512], F32)
        rstd = sbt(k, "rstd", [128, 512], F32)
        memset(k, k.epsb[:], EPS, ["epsb"])
        dma(k, k.cmat[:], cmat_d, (), ["cmat"])
        dma(k, gf[:], gfT, (), ["gf"])
        xv = xT.rearrange("(c p) n -> p c n", p=128)
        yv = yT.rearrange("(c p) n -> p c n", p=128)
        for bi in range(4):
            i = bi % 2
            dma(k, xb[i][:], xv[:, :, bi * 512:(bi + 1) * 512], (), ["xb%d" % i])
            emit_norm_mod(k, xb[i][:], ["xb%d" % i], 512, gf[:], None, ["gf"], lambda c, i=i: yb[i][:, c, :],
                          lambda c, i=i: ["yb%d" % i], sq, "sq", rstd, "rstd")
            dma(k, yv[:, :, bi * 512:(bi + 1) * 512], yb[i][:], ["yb%d" % i], ["yT"])
        k.P.finish("sync")
        k.P.emit()
    return nc


_PROGS = {}


def _prog(name):
    if name not in _PROGS:
        if name == "A":
            _PROGS[name] = build_A()
        elif name == "F":
            _PROGS[name] = build_F()
        else:
            _PROGS[name] = build_B(int(name[1]), debug=("d" in name), stop=int(name.split("s")[1]) if "s" in name else 99)
    return _PROGS[name]


def _fm(v, ncol):
    return np.ascontiguousarray(np.asarray(v, np.float32).reshape(ncol, 128).T)


A_KEYS = ("cT", "w_ada", "b_adaT", "gn1T", "gn2T", "w_in", "gkvT", "w_ukv", "hlb", "lmask", "cmat", "cind", "ropeRT", "rope")
B_KEYS = ("gn1T", "gn2T", "w_in", "gqT", "w_uq", "dlam", "gdnT", "ghnT", "hlb", "lmask", "laminit", "w_out", "w_g", "w_u", "w_d",
          "cmat", "cind", "ropeRT", "rope", "hmask")


def layer_inputs(inp, l, core, hc):
    f32 = np.float32
    c3 = np.stack([inp["c"][0], inp["c"][1], inp["c_ctx"]], 0).astype(f32)
    cT = np.ascontiguousarray(c3.reshape(3, 8, 128).transpose(2, 1, 0))
    lam_init = 0.8 - 0.6 * math.exp(-0.3 * l)
    lmask = np.zeros((128, 4), f32)
    lmask[:, 1:l + 1] = 1.0
    d = dict(
        cT=cT, w_ada=inp["w_ada"][l], b_adaT=_fm(inp["b_ada"][l], 48),
        gn1T=_fm(inp["g_norm1"][l], 8), gn2T=_fm(inp["g_norm2"][l], 8), w_in=inp["w_in"][l],
        gqT=_fm(inp["g_q_norm"][l], 2), w_uq=inp["w_uq"][l], gkvT=_fm(inp["g_kv_norm"][l], 1), w_ukv=inp["w_ukv"][l],
        dlam=np.ascontiguousarray(np.broadcast_to(inp["diff_lambda"][l].reshape(1, 128), (128, 128))).astype(f32),
        gdnT=np.ascontiguousarray(np.tile(inp["g_diff_norm"][l], 2).reshape(128, 1)).astype(f32),
        ghnT=np.ascontiguousarray(np.tile(inp["g_hgrn_norm"][l], 2).reshape(128, 1)).astype(f32),
        hlb=np.ascontiguousarray(np.broadcast_to(inp["hgrn_lower_bounds"].reshape(1, 4, 2, 512), (128, 4, 2, 512))).astype(f32),
        lmask=lmask, laminit=np.tile(np.array([[lam_init, 1.0 - lam_init]], f32), (128, 1)),
        w_out=inp["w_out"][l], w_g=inp["w_ffn_gate"][l], w_u=inp["w_ffn_up"][l], w_d=inp["w_ffn_down"][l],
    )
    d.update(hc)
    d = {kk: np.ascontiguousarray(v, dtype=f32) for kk, v in d.items()}
    return {kk: d[kk] for kk in A_KEYS}, {kk: d[kk] for kk in B_KEYS}


def run(nc, in_maps):
    return run_bass_kernel_spmd(nc, in_maps, core_ids=list(range(NCORES))).results


def kernel(**inp):
    inp = {kk: np.asarray(v) for kk, v in inp.items()}
    x, ctx = inp["x"].astype(np.float32), inp["ctx"].astype(np.float32)
    hcs = [host_consts(c) for c in range(NCORES)]
    xTs = []
    for c in range(NCORES):
        loc = np.concatenate([x[0, c * LAT:(c + 1) * LAT], x[1, c * LAT:(c + 1) * LAT],
                              ctx[0, c * CT:(c + 1) * CT], ctx[1, c * CT:(c + 1) * CT]], 0)
        xTs.append(np.ascontiguousarray(loc.T))
    for l in range(DEPTH):
        lins = [layer_inputs(inp, l, c, hcs[c]) for c in range(NCORES)]
        resA = run(_prog("A"), [dict(lins[c][0], xT=xTs[c]) for c in range(NCORES)])
        exk_all = np.stack([np.asarray(resA[c]["exk"]) for c in range(NCORES)], 0)
        exs_all = np.stack([np.asarray(resA[c]["exs"]) for c in range(NCORES)], 0)
        new = [xt.copy() for xt in xTs]
        for b in range(2):
            resB = run(_prog("B%d" % b), [dict(lins[c][1], xT=xTs[c], exk_all=exk_all, exs_all=exs_all,
                                                modT_i=np.asarray(resA[c]["modT_o"])) for c in range(NCORES)])
            for c in range(NCORES):
                xo = np.asarray(resB[c]["xo"])
                new[c][:, b * LAT:(b + 1) * LAT] = xo[:, 0:LAT]
                new[c][:, 2 * LAT + b * CT:2 * LAT + (b + 1) * CT] = xo[:, LAT:HT]
        xTs = new
    resF = run(_prog("F"), [dict(xT=xTs[c], gfT=_fm(inp["g_final"], 8), cmat=hcs[c]["cmat"]) for c in range(NCORES)])
    out = np.zeros((2, SEQ, D), np.float32)
    for c in range(NCORES):
        yT = np.asarray(resF[c]["yT"])
        out[0, c * LAT:(c + 1) * LAT] = yT[:, 0:LAT].T
        out[1, c * LAT:(c + 1) * LAT] = yT[:, LAT:2 * LAT].T
    return out
```
